# Optimizing a Trainium2 kernel written in Bass

```python
import jax, jax.numpy as jnp
from jax import lax
import numpy as np

D_MODEL = 2048
BATCH = 4
SEQ = 4096
DEPTH = 2

CHUNK = 64
QUERY_BLOCK = 128
D_MIX = D_MODEL
SB_WIDTH = D_MIX // 2
SB_HEADS = 8
SB_HEAD_DIM = SB_WIDTH // SB_HEADS
GLA_VALUE_WIDTH = D_MIX - SB_WIDTH
GLA_HEADS = 4
GLA_KEY_WIDTH = GLA_VALUE_WIDTH // 2
GLA_HEAD_K = GLA_KEY_WIDTH // GLA_HEADS
GLA_HEAD_V = GLA_VALUE_WIDTH // GLA_HEADS
GLA_GATE_RANK = 16
GLA_GATE_TAU = 16.0
NORM_EPS = 1e-6
SPLIT_SIZES = (SB_WIDTH, SB_WIDTH, SB_WIDTH, SB_WIDTH,
               GLA_KEY_WIDTH, GLA_KEY_WIDTH, GLA_VALUE_WIDTH,
               GLA_VALUE_WIDTH, GLA_GATE_RANK)
IN_WIDTH = sum(SPLIT_SIZES)

kernel_name = "hybrid_stickbreaking_gla_block"


def rms_norm(x, gain):
    xf = x.astype(jnp.float32)
    y = xf * lax.rsqrt(jnp.mean(xf * xf, axis=-1, keepdims=True) + NORM_EPS)
    return (y * gain.astype(jnp.float32)).astype(x.dtype)


def to_heads(t, n_heads):
    b, s, w = t.shape
    return t.reshape(b, s, n_heads, w // n_heads).transpose(0, 2, 1, 3)


def from_heads(t):
    b, h, s, d = t.shape
    return t.transpose(0, 2, 1, 3).reshape(b, s, h * d)


def stick_breaking_attention(q, k, v):
    _, _, s_len, d = q.shape
    scale = d ** -0.5
    outs = []
    for i in range(s_len // QUERY_BLOCK):
        start = i * QUERY_BLOCK
        end = start + QUERY_BLOCK
        qb = q[:, :, start:end].astype(jnp.float32)
        kb = k[:, :, :end].astype(jnp.float32)
        vb = v[:, :, :end].astype(jnp.float32)
        z = jnp.einsum('bhqd,bhkd->bhqk', qb, kb) * scale
        t_idx = start + jnp.arange(QUERY_BLOCK)[:, None]
        s_idx = jnp.arange(end)[None, :]
        mask = s_idx < t_idx
        log_fail = jnp.where(mask, jax.nn.log_sigmoid(-z), 0.0)
        later = lax.cumsum(log_fail, axis=3, reverse=True) - log_fail
        w = jnp.where(mask, jnp.exp(jax.nn.log_sigmoid(z) + later), 0.0)
        outs.append(jnp.einsum('bhqk,bhkd->bhqd', w, vb))
    return jnp.concatenate(outs, axis=2).astype(v.dtype)


def gated_linear_attention(q, k, v, log_alpha):
    b, h, s_len, dk = q.shape
    dv = v.shape[-1]
    n_c = s_len // CHUNK
    qf = (q.astype(jnp.float32) * dk ** -0.5).reshape(b, h, n_c, CHUNK, dk)
    kf = k.astype(jnp.float32).reshape(b, h, n_c, CHUNK, dk)
    vf = v.astype(jnp.float32).reshape(b, h, n_c, CHUNK, dv)
    cum = jnp.cumsum(log_alpha.astype(jnp.float32).reshape(b, h, n_c, CHUNK, dk), axis=3)
    q_dec = qf * jnp.exp(cum)
    k_inv = kf * jnp.exp(-cum)
    scores = jnp.einsum('bhncd,bhned->bhnce', q_dec, k_inv)
    causal = jnp.arange(CHUNK)[None, :] <= jnp.arange(CHUNK)[:, None]
    scores = jnp.where(causal, scores, 0.0)
    o_intra = jnp.einsum('bhnce,bhnev->bhncv', scores, vf)
    cum_last = cum[:, :, :, -1:, :]
    kv = jnp.einsum('bhncd,bhncv->bhndv', kf * jnp.exp(cum_last - cum), vf)
    chunk_decay = jnp.exp(cum_last[:, :, :, 0, :])

    def step(state, inp):
        dec, kv_c = inp
        return dec[..., None] * state + kv_c, state

    init = jnp.zeros((b, h, dk, dv), jnp.float32)
    _, states = lax.scan(step, init, (jnp.moveaxis(chunk_decay, 2, 0), jnp.moveaxis(kv, 2, 0)))
    states = jnp.moveaxis(states, 0, 2)
    o_inter = jnp.einsum('bhncd,bhndv->bhncv', q_dec, states)
    return (o_intra + o_inter).reshape(b, h, s_len, dv).astype(v.dtype)


def setup_inputs(seed: int = 0) -> dict:
    key = jax.random.key(seed)
    ks = jax.random.split(key, 11)
    f32 = jnp.float32
    x = jax.random.normal(ks[0], (BATCH, SEQ, D_MODEL), f32)
    norm_gain = 1.0 + 0.02 * jax.random.normal(ks[1], (DEPTH, D_MODEL), f32)
    w_in = jax.random.normal(ks[2], (DEPTH, D_MODEL, IN_WIDTH), f32) * D_MODEL ** -0.5
    sb_q_gain = 1.0 + 0.02 * jax.random.normal(ks[3], (DEPTH, SB_HEAD_DIM), f32)
    sb_k_gain = 1.0 + 0.02 * jax.random.normal(ks[4], (DEPTH, SB_HEAD_DIM), f32)
    sb_o_gain = 1.0 + 0.02 * jax.random.normal(ks[5], (DEPTH, SB_WIDTH), f32)
    gla_w_alpha = jax.random.normal(ks[6], (DEPTH, GLA_GATE_RANK, GLA_KEY_WIDTH), f32) * GLA_GATE_RANK ** -0.5
    gla_b_alpha = 0.1 * jax.random.normal(ks[7], (DEPTH, GLA_KEY_WIDTH), f32)
    gla_o_gain = 1.0 + 0.02 * jax.random.normal(ks[8], (DEPTH, GLA_VALUE_WIDTH), f32)
    w_out = jax.random.normal(ks[9], (DEPTH, D_MIX, D_MODEL), f32) * (0.5 * D_MIX ** -0.5)
    return {"x": x, "norm_gain": norm_gain, "w_in": w_in, "sb_q_gain": sb_q_gain,
            "sb_k_gain": sb_k_gain, "sb_o_gain": sb_o_gain, "gla_w_alpha": gla_w_alpha,
            "gla_b_alpha": gla_b_alpha, "gla_o_gain": gla_o_gain, "w_out": w_out}


def reference(x, norm_gain, w_in, sb_q_gain, sb_k_gain, sb_o_gain, gla_w_alpha,
              gla_b_alpha, gla_o_gain, w_out):
    offsets = np.cumsum(SPLIT_SIZES)[:-1].tolist()
    for layer in range(DEPTH):
        h = rms_norm(x, norm_gain[layer])
        proj = jnp.einsum('bsd,de->bse', h, w_in[layer])
        (sb_q, sb_k, sb_v, sb_g, gl_q, gl_k, gl_v, gl_g, gl_lr) = jnp.split(proj, offsets, axis=-1)

        q = rms_norm(to_heads(sb_q, SB_HEADS), sb_q_gain[layer])
        k = rms_norm(to_heads(sb_k, SB_HEADS), sb_k_gain[layer])
        v = to_heads(sb_v, SB_HEADS)
        a_out = stick_breaking_attention(q, k, v)
        a_out = rms_norm(a_out, sb_o_gain[layer].reshape(SB_HEADS, 1, SB_HEAD_DIM))
        a_out = from_heads(a_out) * jax.nn.silu(sb_g)

        gate_logits = jnp.einsum('bsr,rk->bsk', gl_lr, gla_w_alpha[layer]) + gla_b_alpha[layer]
        log_alpha = jax.nn.log_sigmoid(gate_logits.astype(jnp.float32)) / GLA_GATE_TAU
        b_out = gated_linear_attention(to_heads(gl_q, GLA_HEADS), to_heads(gl_k, GLA_HEADS),
                                       to_heads(gl_v, GLA_HEADS), to_heads(log_alpha, GLA_HEADS))
        b_out = rms_norm(b_out, gla_o_gain[layer].reshape(GLA_HEADS, 1, GLA_HEAD_V))
        b_out = from_heads(b_out) * jax.nn.silu(gl_g)

        mixed = jnp.concatenate([a_out, b_out], axis=-1)
        x = x + jnp.einsum('bse,ed->bsd', mixed, w_out[layer]).astype(x.dtype)
    return x
```

```python
import numpy as np
import concourse.bass as bass
import concourse.mybir as mybir
from contextlib import ExitStack

F32, BF16 = mybir.dt.float32, mybir.dt.bfloat16
AF = mybir.ActivationFunctionType
ALU = mybir.AluOpType
EPS = 1e-6
ENGS = ("pe", "act", "dve", "pool", "sp")


class Sched:
    def __init__(self, nc):
        self.nc = nc
        self.ops = {e: [] for e in ENGS}
        self.last_w = {}
        self.readers = {}
        self.dma_keys = {}
        self.all_ops = []

    def op(self, eng, fn, reads=(), writes=(), dma_key=None, extra_deps=()):
        o = dict(eng=eng, idx=len(self.ops[eng]), fn=fn, deps=[], dma_key=dma_key,
                 dma_cnt=None, needed=False, cnt=None, inc=False)
        if dma_key is not None:
            self.dma_keys[dma_key] = self.dma_keys.get(dma_key, 0) + 1
            o["dma_cnt"] = self.dma_keys[dma_key]
        deps = []
        for b in reads:
            w = self.last_w.get(b)
            if w is not None:
                deps.append(w)
        for b in writes:
            w = self.last_w.get(b)
            if w is not None:
                deps.append(w)
            deps.extend(self.readers.get(b, ()))
        deps.extend(extra_deps)
        seen = set()
        for d in deps:
            if d is o or id(d) in seen:
                continue
            seen.add(id(d))
            if d["dma_key"] is None and d["eng"] == eng and eng == "pe":
                continue
            o["deps"].append(d)
        for b in reads:
            self.readers.setdefault(b, []).append(o)
        for b in writes:
            self.last_w[b] = o
            self.readers[b] = []
        self.ops[eng].append(o)
        self.all_ops.append(o)
        return o

    def barrier(self):
        lasts = []
        for e in ENGS:
            for o in reversed(self.ops[e]):
                if o["fn"] is not None and o["dma_key"] is None:
                    lasts.append(o)
                    break
        lastdma = {}
        for o in self.all_ops:
            if o["dma_key"] is not None:
                lastdma[o["dma_key"]] = o
        deps = lasts + list(lastdma.values())
        for e in ENGS:
            self.op(e, None, extra_deps=[d for d in deps if not (d["eng"] == e and d["dma_key"] is None)])
        self.last_w = {}
        self.readers = {}

    def flush(self, get_sem):
        nc = self.nc
        if not hasattr(self, "sems"):
            self.sems = {}
            self.eng_cnt = {e: 0 for e in ENGS}
            self.seen = {e: {} for e in ENGS}
        for o in self.all_ops:
            for d in o["deps"]:
                d["needed"] = True
        for e in ENGS:
            c = self.eng_cnt[e]
            for o in self.ops[e]:
                if o["dma_key"] is None and o["needed"]:
                    assert o["fn"] is not None
                    c += 1
                    o["cnt"] = c
                    o["inc"] = True
            self.eng_cnt[e] = c
        sems = self.sems
        for e in ENGS:
            if ("eng", e) not in sems:
                sems[("eng", e)] = get_sem("e_" + e)
        for k in self.dma_keys:
            if ("dma", k) not in sems:
                sems[("dma", k)] = get_sem("d_" + str(k))
        seen = self.seen

        def run_engine(e, eng):
            for o in self.ops[e]:
                need = {}
                for d in o["deps"]:
                    if d["dma_key"] is not None:
                        key, val = ("dma", d["dma_key"]), 16 * d["dma_cnt"]
                    else:
                        key, val = ("eng", d["eng"]), d["cnt"]
                    if need.get(key, 0) < val:
                        need[key] = val
                for key, val in need.items():
                    if seen[e].get(key, 0) >= val:
                        continue
                    seen[e][key] = val
                    eng.wait_ge(sems[key], val)
                if o["fn"] is None:
                    continue
                ins = o["fn"](eng)
                if o["dma_key"] is not None:
                    ins.then_inc(sems[("dma", o["dma_key"])], 16)
                elif o["inc"]:
                    ins.then_inc(sems[("eng", e)], 1)

        with nc.Block() as block:
            @block.tensor
            def _(eng):
                run_engine("pe", eng)

            @block.scalar
            def _(eng):
                run_engine("act", eng)

            @block.vector
            def _(eng):
                run_engine("dve", eng)

            @block.gpsimd
            def _(eng):
                run_engine("pool", eng)

            @block.sync
            def _(eng):
                run_engine("sp", eng)
        self.n_ops = getattr(self, "n_ops", 0) + len(self.all_ops)
        self.ops = {e: [] for e in ENGS}
        self.all_ops = []
        self.last_w = {}
        self.readers = {}


class HT:
    def __init__(self, parts):
        self.parts = parts

    def at(self, kc, lo, hi):
        return self.parts[kc // 8][:, kc % 8, lo:hi]

    def grp(self, a, kn, lo, hi):
        return self.parts[a][:, 0:kn, lo:hi]


class Rot:
    def __init__(self, n):
        self.n, self.i = n, 0

    def next(self):
        r = self.i % self.n
        self.i += 1
        return r


class Cfg:
    def __init__(self, D=2048, S=4096, NSB=4, NGL=2, DOUT=2048, TB=2048):
        self.D, self.S, self.NSB, self.NGL = D, S, NSB, NGL
        self.KC = D // 128
        self.NT = S // 128
        self.NP = S // 512
        self.FM = 3 * NSB + 4 * NGL
        self.C_SBV = 128 * self.FM
        self.C_GLV = self.C_SBV + 128 * NSB
        self.C_LR = self.C_GLV + 256 * NGL
        self.NCOL = self.C_LR + 16
        self.MIXW = 128 * NSB + 256 * NGL
        self.DOUT = DOUT
        self.TB = TB
        self.FC = 2 * self.MIXW // 128


def make_consts():
    c = {}
    c["ident"] = np.eye(128, dtype=np.float32)
    c["ones"] = np.ones((128, 128), np.float32)
    q = np.arange(128)[:, None]
    k = np.arange(128)[None, :]
    c["maskb"] = np.where(k < q, 1e30, -1e30).astype(np.float32)
    same = (q // 64) == (k // 64)
    c["bt"] = (same & (q <= k)).astype(np.float32)
    c["maskt"] = (same & (q <= k)).astype(np.float32)
    return np.concatenate([c["ident"], c["ones"], c["maskb"], c["bt"], c["maskt"]], axis=1)


class Builder:
    def __init__(self, nc, cfg, es):
        self.nc, self.cfg, self.es = nc, cfg, es
        self.S = Sched(nc)
        self.uid = 0

    def sb(self, name, shape, dt):
        return self.es.enter_context(self.nc.sbuf_tensor(name, shape, dt))

    def ps(self, name, shape, dt):
        return self.es.enter_context(self.nc.psum_tensor(name, shape, dt))

    def dram(self, name, shape, dt, kind="Internal"):
        return self.nc.dram_tensor(name, shape, dt, kind=kind).ap()

    def alloc_common(self):
        nc, cfg, S = self.nc, self.cfg, self.S
        self.cst = self.sb("cst", [128, 640], F32)
        self.identb = self.sb("identb", [128, 128], BF16)
        self.maskt_b = None
        self.banks = [self.ps(f"bank{i}", [128, 512], F32) for i in range(6)]
        self.bankT = [self.ps(f"bankT{i}", [128, 1024], BF16) for i in range(2)]
        self.ident = self.cst[:, 0:128]
        self.ones = self.cst[:, 128:256]
        self.maskb = self.cst[:, 256:384]
        self.bt = self.cst[:, 384:512]
        self.maskt = self.cst[:, 512:640]

    def load_consts(self, consts_ap):
        S = self.S
        S.op("sp", lambda g: g.dma_start(out=self.cst[:], in_=consts_ap), writes=["cst"], dma_key="cst")
        S.op("dve", lambda g: g.tensor_copy(out=self.identb[:], in_=self.ident), reads=["cst"], writes=["identb"])

    def phase_norm(self, L, x_ap, gain_ap, hT):
        nc, cfg, S = self.nc, self.cfg, self.S
        D, KC, NT = cfg.D, cfg.KC, cfg.NT
        gbc, xt, hb, junk, st = self.gbc, self.xt, self.hb, self.junk, self.nstat
        S.op("sp", lambda g: g.dma_start(out=gbc[:], in_=gain_ap.partition_broadcast(128)),
             writes=["gbc"], dma_key="gbc")
        rx = Rot(len(xt))
        rh = Rot(len(hb))
        for t in range(NT):
            xs, hs = rx.next(), rh.next()
            S.op("sp", lambda g, xs=xs, t=t: g.dma_start(out=xt[xs][:], in_=x_ap[t * 128:(t + 1) * 128, :]),
                 writes=[f"xt{xs}"], dma_key=f"xt{xs}")
            S.op("act", lambda g, xs=xs, t=t: g.activation(out=junk[:], in_=xt[xs][:], func=AF.Square,
                                                          accum_out=st[:, 3 * t:3 * t + 1]),
                 reads=[f"xt{xs}"], writes=["junk", f"nst{t}"])
            S.op("act", lambda g, t=t: g.activation(out=st[:, 3 * t + 1:3 * t + 2], in_=st[:, 3 * t:3 * t + 1],
                                                    func=AF.Sqrt, scale=1.0 / D, bias=self.epsc[:]),
                 reads=[f"nst{t}", "epsc"], writes=[f"nst{t}"])
            S.op("dve", lambda g, t=t: g.reciprocal(out=st[:, 3 * t + 2:3 * t + 3], in_=st[:, 3 * t + 1:3 * t + 2]),
                 reads=[f"nst{t}"], writes=[f"nst{t}"])
            S.op("dve", lambda g, xs=xs, hs=hs, t=t: g.scalar_tensor_tensor(
                out=hb[hs][:], in0=xt[xs][:], scalar=st[:, 3 * t + 2:3 * t + 3], in1=gbc[:],
                op0=ALU.mult, op1=ALU.mult),
                reads=[f"xt{xs}", f"nst{t}", "gbc"], writes=[f"hb{hs}"])
            ngrp = (KC + 7) // 8
            for a in range(ngrp):
                k0 = a * 8
                kn = min(8, KC - k0)
                bT = self.bankT[a % 2]
                for j in range(kn):
                    S.op("pe", lambda g, bT=bT, j=j, hs=hs, k0=k0: g.transpose(
                        out=bT[:, j * 128:(j + 1) * 128], in_=hb[hs][:, (k0 + j) * 128:(k0 + j + 1) * 128],
                        identity=self.identb[:]),
                        reads=[f"hb{hs}", "identb"], writes=[f"bankT{a % 2}"])
                src = bT[:, 0:kn * 128].rearrange("p (k c) -> p k c", c=128)
                dst = hT.grp(a, kn, t * 128, (t + 1) * 128)
                if a % 2 == 0:
                    S.op("act", lambda g, src=src, dst=dst: g.copy(out=dst, in_=src),
                         reads=[f"bankT{a % 2}"], writes=[f"hT{t}"])
                else:
                    S.op("dve", lambda g, src=src, dst=dst: g.tensor_copy(out=dst, in_=src),
                         reads=[f"bankT{a % 2}"], writes=[f"hT{t}"])

    def phase_inproj(self, L, win_ap, hT, scr, gq, gk):
        nc, cfg, S = self.nc, self.cfg, self.S
        KC, NP, NT, NSB, NGL, FM = cfg.KC, cfg.NP, cfg.NT, cfg.NSB, cfg.NGL, cfg.FM
        wb = self.wb
        rw = Rot(len(wb))
        rbank = Rot(4)
        rb2 = Rot(2)
        rsq = Rot(len(self.sq))
        rstb = Rot(len(self.stgb))
        rstf = Rot(len(self.stgf))

        def load_w(c0, w):
            s = rw.next()
            step = 4
            for k0 in range(0, KC, step):
                kn = min(step, KC - k0)
                src = win_ap[k0 * 128:(k0 + kn) * 128, c0:c0 + w].rearrange("(k p) c -> p k c", p=128)
                S.op("pool", lambda g, s=s, k0=k0, kn=kn, src=src, w=w: g.dma_start(
                    out=wb[s][:, k0:k0 + kn, 0:w], in_=src),
                    writes=[f"wb{s}"], dma_key=f"wb{s}")
            return s

        groups = []
        for i in range(NSB):
            groups += [("sbq", i), ("sbk", i), ("sbg", i)]
        for j in range(NGL):
            groups += [("glq", j), ("glk", j), ("glg", 2 * j), ("glg", 2 * j + 1)]
        assert len(groups) == FM
        blocks = []
        for b0 in range(0, FM, 4):
            blocks.append(("fm", b0, min(4, FM - b0)))
        blocks.append(("sbv", cfg.C_SBV, 128 * NSB))
        blocks.append(("glv", cfg.C_GLV, 256 * NGL))
        blocks.append(("lr", cfg.C_LR, 16))

        def start_load(blk):
            if blk[0] == "fm":
                return load_w(blk[1] * 128, blk[2] * 128)
            return load_w(blk[1], blk[2])

        def store(kind_eng, fn_stage, dst_ap, stg_name, stg_ap, key_deps):
            pass

        slot_next = start_load(blocks[0])
        for bi, blk in enumerate(blocks):
            slot = slot_next
            if bi + 1 < len(blocks):
                slot_next = start_load(blocks[bi + 1])
            if blk[0] == "fm":
                for gl in range(blk[2]):
                    kind, idx = groups[blk[1] + gl]
                    for p in range(NP):
                        bk = rbank.next()
                        bank = self.banks[bk]
                        hnames = [f"hT{t}" for t in range(4 * p, 4 * p + 4)]
                        for kc in range(KC):
                            S.op("pe", lambda g, bank=bank, slot=slot, kc=kc, gl=gl, p=p: g.matmul(
                                bank[:, 0:512], lhsT=wb[slot][:, kc, gl * 128:(gl + 1) * 128],
                                rhs=hT.at(kc, p * 512, (p + 1) * 512), start=(kc == 0), stop=(kc == KC - 1)),
                                reads=[f"wb{slot}"] + hnames, writes=[f"bank{bk}"])
                        tsl = slice(p * 512, (p + 1) * 512)
                        if kind in ("sbq", "sbk"):
                            sq = rsq.next()
                            b2 = 4 + rb2.next()
                            S.op("act", lambda g, sq=sq, bank=bank: g.activation(out=self.sq[sq][:], in_=bank[:, 0:512], func=AF.Square),
                                 reads=[f"bank{bk}"], writes=[f"sq{sq}"])
                            S.op("pe", lambda g, sq=sq, b2=b2: g.matmul(self.banks[b2][:, 0:512], lhsT=self.ones, rhs=self.sq[sq][:],
                                                                       start=True, stop=True),
                                 reads=[f"sq{sq}", "cst"], writes=[f"bank{b2}"])
                            S.op("act", lambda g, sq=sq, b2=b2: g.activation(out=self.sq[sq][:], in_=self.banks[b2][:, 0:512],
                                                                           func=AF.Sqrt, scale=1.0 / 128, bias=self.epsc[:]),
                                 reads=[f"bank{b2}", "epsc"], writes=[f"sq{sq}"])
                            S.op("dve", lambda g, sq=sq: g.reciprocal(out=self.sq[sq][:], in_=self.sq[sq][:]),
                                 reads=[f"sq{sq}"], writes=[f"sq{sq}"])
                            st = rstb.next()
                            gv = gq if kind == "sbq" else gk
                            S.op("dve", lambda g, st=st, bank=bank, sq=sq, gv=gv: g.scalar_tensor_tensor(
                                out=self.stgb[st][:], in0=bank[:, 0:512], scalar=gv, in1=self.sq[sq][:],
                                op0=ALU.mult, op1=ALU.mult),
                                reads=[f"bank{bk}", f"sq{sq}", "gains"], writes=[f"stgb{st}"])
                            dst = (scr["qT"] if kind == "sbq" else scr["kT"])[idx][:, tsl]
                            S.op("sp", lambda g, st=st, dst=dst: g.dma_start(out=dst, in_=self.stgb[st][:]),
                                 reads=[f"stgb{st}"], writes=[f"scr_{kind}{idx}_{p}"], dma_key=f"stgb{st}")
                        elif kind in ("sbg", "glg"):
                            st = rstb.next()
                            S.op("act", lambda g, st=st, bank=bank: g.activation(out=self.stgb[st][:], in_=bank[:, 0:512], func=AF.Silu),
                                 reads=[f"bank{bk}"], writes=[f"stgb{st}"])
                            dst = (scr["sgT"] if kind == "sbg" else scr["gsgT"])[idx][:, tsl]
                            S.op("sp", lambda g, st=st, dst=dst: g.dma_start(out=dst, in_=self.stgb[st][:]),
                                 reads=[f"stgb{st}"], writes=[f"scr_{kind}{idx}_{p}"], dma_key=f"stgb{st}")
                        else:
                            st = rstf.next()
                            S.op("dve", lambda g, st=st, bank=bank: g.tensor_copy(out=self.stgf[st][:], in_=bank[:, 0:512]),
                                 reads=[f"bank{bk}"], writes=[f"stgf{st}"])
                            dst = (scr["gqT"] if kind == "glq" else scr["gkT"])[idx][:, tsl]
                            S.op("sp", lambda g, st=st, dst=dst: g.dma_start(out=dst, in_=self.stgf[st][:]),
                                 reads=[f"stgf{st}"], writes=[f"scr_{kind}{idx}_{p}"], dma_key=f"stgf{st}")
            elif blk[0] in ("sbv", "glv"):
                w = blk[2]
                for t in range(NT):
                    bk = rbank.next()
                    bank = self.banks[bk]
                    for kc in range(KC):
                        S.op("pe", lambda g, bank=bank, slot=slot, kc=kc, t=t, w=w: g.matmul(
                            bank[:, 0:w], lhsT=hT.at(kc, t * 128, (t + 1) * 128), rhs=wb[slot][:, kc, 0:w],
                            start=(kc == 0), stop=(kc == KC - 1)),
                            reads=[f"wb{slot}", f"hT{t}"], writes=[f"bank{bk}"])
                    st = rstb.next()
                    if t % 2 == 0:
                        S.op("act", lambda g, st=st, bank=bank, w=w: g.copy(out=self.stgb[st][:, 0:w], in_=bank[:, 0:w]),
                             reads=[f"bank{bk}"], writes=[f"stgb{st}"])
                    else:
                        S.op("dve", lambda g, st=st, bank=bank, w=w: g.tensor_copy(out=self.stgb[st][:, 0:w], in_=bank[:, 0:w]),
                             reads=[f"bank{bk}"], writes=[f"stgb{st}"])
                    dst = scr[blk[0]][:, t, :]
                    S.op("sp", lambda g, st=st, dst=dst, w=w: g.dma_start(out=dst, in_=self.stgb[st][:, 0:w]),
                         reads=[f"stgb{st}"], writes=[f"scr_{blk[0]}_{t}"], dma_key=f"stgb{st}")
            else:
                for p in range(NP):
                    bk = rbank.next()
                    bank = self.banks[bk]
                    hnames = [f"hT{t}" for t in range(4 * p, 4 * p + 4)]
                    for kc in range(KC):
                        S.op("pe", lambda g, bank=bank, slot=slot, kc=kc, p=p: g.matmul(
                            bank[0:16, 0:512], lhsT=wb[slot][:, kc, 0:16],
                            rhs=hT.at(kc, p * 512, (p + 1) * 512), start=(kc == 0), stop=(kc == KC - 1)),
                            reads=[f"wb{slot}"] + hnames, writes=[f"bank{bk}"])
                    st = rstf.next()
                    S.op("dve", lambda g, st=st, bank=bank: g.tensor_copy(out=self.stgf[st][0:16, :], in_=bank[0:16, 0:512]),
                         reads=[f"bank{bk}"], writes=[f"stgf{st}"])
                    dst = scr["lrT"][:, p * 512:(p + 1) * 512]
                    S.op("sp", lambda g, st=st, dst=dst: g.dma_start(out=dst, in_=self.stgf[st][0:16, :]),
                         reads=[f"stgf{st}"], writes=[f"scr_lr_{p}"], dma_key=f"stgf{st}")

    def phase_sb(self, L, scr, go_col, mixT_ap):
        nc, cfg, S = self.nc, self.cfg, self.S
        NT, NP, NSB = cfg.NT, cfg.NP, cfg.NSB
        Sq = cfg.S
        vsb = self.vsb
        S.op("sp", lambda g: g.dma_start(out=vsb[:, :, 0:128 * NSB], in_=scr["sbv"]),
             reads=[f"scr_sbv_{t}" for t in range(NT)], writes=["vsb"], dma_key="vsb")
        rq = Rot(2)
        rcs = Rot(len(self.cs))
        rw = Rot(len(self.wt))
        rz = Rot(3)
        re = Rot(len(self.ebuf))
        rwT = Rot(len(self.wT))
        rT = Rot(2)
        rpo = Rot(1)
        items = []

        head_slots = {}

        def load_head(h):
            s = rq.next()
            head_slots[h] = s
            S.op("sp", lambda g, s=s, h=h: g.dma_start(out=self.qn[s][:], in_=scr["qT"][h]),
                 reads=[f"scr_sbq{h}_{p}" for p in range(NP)], writes=[f"qn{s}"], dma_key=f"qn{s}")
            S.op("sp", lambda g, s=s, h=h: g.dma_start(out=self.kn[s][:], in_=scr["kT"][h]),
                 reads=[f"scr_sbk{h}_{p}" for p in range(NP)], writes=[f"kn{s}"], dma_key=f"kn{s}")
            S.op("sp", lambda g, s=s, h=h: g.dma_start(out=self.sg[s][:], in_=scr["sgT"][h]),
                 reads=[f"scr_sbg{h}_{p}" for p in range(NP)], writes=[f"sg{s}"], dma_key=f"sg{s}")

        def stage_a(h, i):
            s = head_slots[h]
            K = (i + 1) * 128
            c = rcs.next()
            cs = self.cs[c]
            npieces = (K + 511) // 512
            S.op("pool", lambda g, cs=cs: g.memset(cs[:, 0:1], 0.0), writes=[f"cs{c}"])
            for p in range(npieces):
                off = p * 512
                wp = min(512, K - off)
                zb = rz.next()
                zbank = self.banks[zb]
                S.op("pe", lambda g, zbank=zbank, s=s, i=i, off=off, wp=wp: g.matmul(
                    zbank[:, 0:wp], lhsT=self.qn[s][:, i * 128:(i + 1) * 128], rhs=self.kn[s][:, off:off + wp],
                    start=True, stop=True),
                    reads=[f"qn{s}", f"kn{s}"], writes=[f"bank{zb}"])
                eb = re.next()
                ebuf = self.ebuf[eb]
                S.op("act", lambda g, ebuf=ebuf, zbank=zbank, wp=wp: g.activation(
                    out=ebuf[:, 0:wp], in_=zbank[:, 0:wp], func=AF.Exp, scale=-1.0),
                    reads=[f"bank{zb}"], writes=[f"ebuf{eb}"])
                S.op("act", lambda g, ebuf=ebuf, wp=wp: g.activation(
                    out=ebuf[:, 0:wp], in_=ebuf[:, 0:wp], func=AF.Ln, bias=self.onec[:], scale=1.0),
                    reads=[f"ebuf{eb}", "onec"], writes=[f"ebuf{eb}"])
                S.op("dve", lambda g, cs=cs, zbank=zbank, ebuf=ebuf, off=off, wp=wp: g.tensor_tensor_scan(
                    out=cs[:, off + 1:off + 1 + wp], data0=zbank[:, 0:wp], data1=ebuf[:, 0:wp],
                    initial=cs[:, off:off + 1], op0=ALU.add, op1=ALU.add),
                    reads=[f"bank{zb}", f"ebuf{eb}", f"cs{c}"], writes=[f"cs{c}"])
                if p == npieces - 1:
                    S.op("dve", lambda g, cs=cs, i=i, c=c: g.scalar_tensor_tensor(
                        out=self.djunk[:], in0=cs[:, i * 128:(i + 1) * 128], scalar=-1.0, in1=self.ident,
                        op0=ALU.mult, op1=ALU.mult, accum_out=self.ntot[:, c:c + 1]),
                        reads=[f"cs{c}", "cst"], writes=["djunk", f"ntot{c}"])
                S.op("dve", lambda g, cs=cs, zbank=zbank, off=off, wp=wp: g.tensor_tensor(
                    out=cs[:, off:off + wp], in0=zbank[:, 0:wp], in1=cs[:, off:off + wp], op=ALU.add),
                    reads=[f"bank{zb}", f"cs{c}"], writes=[f"cs{c}"])
            S.op("dve", lambda g, cs=cs, i=i: g.tensor_tensor(
                out=cs[:, i * 128:(i + 1) * 128], in0=cs[:, i * 128:(i + 1) * 128], in1=self.maskb, op=ALU.min),
                reads=[f"cs{c}", "cst"], writes=[f"cs{c}"])
            wsl = rw.next()
            S.op("act", lambda g, cs=cs, wsl=wsl, K=K, c=c: g.activation(
                out=self.wt[wsl][:, 0:K], in_=cs[:, 0:K], func=AF.Exp, bias=self.ntot[:, c:c + 1], scale=1.0),
                reads=[f"cs{c}", f"ntot{c}"], writes=[f"wt{wsl}"])
            return dict(h=h, i=i, wsl=wsl, s=s)

        def stage_b(st):
            h, i, wsl, s = st["h"], st["i"], st["wsl"], st["s"]
            nkb = i + 1
            po = self.banks[3]
            qcol = (i % 4) * 128
            for g0 in range(0, nkb, 4):
                gn = min(4, nkb - g0)
                tb = rT.next()
                bT = self.bankT[tb]
                for j in range(gn):
                    kb = g0 + j
                    S.op("pe", lambda g, bT=bT, j=j, kb=kb, wsl=wsl: g.transpose(
                        out=bT[:, j * 128:(j + 1) * 128], in_=self.wt[wsl][:, kb * 128:(kb + 1) * 128],
                        identity=self.identb[:]),
                        reads=[f"wt{wsl}", "identb"], writes=[f"bankT{tb}"])
                ws = rwT.next()
                S.op("act", lambda g, bT=bT, ws=ws, gn=gn: g.copy(out=self.wT[ws][:, 0:gn * 128], in_=bT[:, 0:gn * 128]),
                     reads=[f"bankT{tb}"], writes=[f"wT{ws}"])
                for j in range(gn):
                    kb = g0 + j
                    S.op("pe", lambda g, po=po, ws=ws, j=j, kb=kb, h=h, qcol=qcol, nkb=nkb: g.matmul(
                        po[:, qcol:qcol + 128], lhsT=vsb[:, kb, h * 128:(h + 1) * 128],
                        rhs=self.wT[ws][:, j * 128:(j + 1) * 128], start=(kb == 0), stop=(kb == nkb - 1)),
                        reads=[f"wT{ws}", "vsb"], writes=["bank3"])
            if i % 4 == 3 or i == NT - 1:
                nq = (i % 4 + 1) * 128
                q0 = (i // 4) * 512
                S.op("act", lambda g, po=po, nq=nq: g.activation(out=self.sqo[:, 0:nq], in_=po[:, 0:nq], func=AF.Square),
                     reads=["bank3"], writes=["sqo"])
                S.op("pe", lambda g, nq=nq: g.matmul(self.banks[5][:, 0:nq], lhsT=self.ones, rhs=self.sqo[:, 0:nq],
                                                     start=True, stop=True),
                     reads=["sqo", "cst"], writes=["bank5"])
                S.op("act", lambda g, nq=nq: g.activation(out=self.sqo[:, 0:nq], in_=self.banks[5][:, 0:nq], func=AF.Sqrt,
                                                          scale=1.0 / 128, bias=self.epsc[:]),
                     reads=["bank5", "epsc"], writes=["sqo"])
                S.op("dve", lambda g, nq=nq: g.reciprocal(out=self.sqo[:, 0:nq], in_=self.sqo[:, 0:nq]),
                     reads=["sqo"], writes=["sqo"])
                S.op("dve", lambda g, po=po, nq=nq, h=h: g.scalar_tensor_tensor(
                    out=self.sqo[:, 0:nq], in0=po[:, 0:nq], scalar=go_col[:, h:h + 1], in1=self.sqo[:, 0:nq],
                    op0=ALU.mult, op1=ALU.mult),
                    reads=["bank3", "sqo", "gains"], writes=["sqo"])
                ms = self.rmix.next()
                S.op("dve", lambda g, nq=nq, q0=q0, s=s, ms=ms: g.tensor_tensor(
                    out=self.mixs[ms][:, 0:nq], in0=self.sqo[:, 0:nq], in1=self.sg[s][:, q0:q0 + nq], op=ALU.mult),
                    reads=["sqo", f"sg{s}"], writes=[f"mixs{ms}"])
                S.op("sp", lambda g, nq=nq, q0=q0, h=h, ms=ms: g.dma_start(
                    out=mixT_ap[h * 128:(h + 1) * 128, q0:q0 + nq], in_=self.mixs[ms][:, 0:nq]),
                    reads=[f"mixs{ms}"], writes=[f"mixT_sb{h}_{q0}"], dma_key=f"mixs{ms}")

        work = [(h, i) for h in range(NSB) for i in range(NT)]
        load_head(0)
        if NSB > 1:
            load_head(1)
        pend = None
        for (h, i) in work:
            st = stage_a(h, i)
            if pend is not None:
                stage_b(pend)
            pend = st
            if i == 0 and h >= 1 and h + 1 < NSB:
                load_head(h + 1)
        stage_b(pend)

    def phase_gla(self, L, scr, wal, gog_col, mixT_ap):
        nc, cfg, S = self.nc, self.cfg, self.S
        NT, NP, NSB, NGL = cfg.NT, cfg.NP, cfg.NSB, cfg.NGL
        gvs = self.vsb
        S.op("sp", lambda g: g.dma_start(out=gvs[:, :, 0:256 * NGL], in_=scr["glv"]),
             reads=[f"scr_glv_{t}" for t in range(NT)], writes=["vsb"], dma_key="vsb")
        for a in range(len(self.lra)):
            S.op("pool", lambda g, a=a: g.memset(self.lra[a][:], 1.0), writes=[f"lra{a}"])
        rl = Rot(len(self.lra))
        rg = Rot(len(self.gq))
        rtmp = Rot(len(self.gt))
        rmix = self.rmix
        for j in range(NGL):
            S.op("pool", lambda g: g.memset(self.Sf[0][:], 0.0), writes=["Sf0"])
            S.op("pool", lambda g: g.memset(self.Sb[0][:], 0.0), writes=["Sb0"])
            cur = 0
            for p in range(NP):
                ls = rl.next()
                gs = rg.next()
                psl = slice(p * 512, (p + 1) * 512)
                S.op("sp", lambda g, ls=ls, psl=psl: g.dma_start(out=self.lra[ls][0:16, :], in_=scr["lrT"][:, psl]),
                     reads=[f"scr_lr_{p}"], writes=[f"lra{ls}"], dma_key=f"lra{ls}")
                S.op("sp", lambda g, gs=gs, psl=psl, j=j: g.dma_start(out=self.gq[gs][:], in_=scr["gqT"][j][:, psl]),
                     reads=[f"scr_glq{j}_{p}"], writes=[f"gq{gs}"], dma_key=f"gq{gs}")
                S.op("sp", lambda g, gs=gs, psl=psl, j=j: g.dma_start(out=self.gk[gs][:], in_=scr["gkT"][j][:, psl]),
                     reads=[f"scr_glk{j}_{p}"], writes=[f"gk{gs}"], dma_key=f"gk{gs}")
                for vh in range(2):
                    S.op("sp", lambda g, gs=gs, psl=psl, j=j, vh=vh: g.dma_start(
                        out=self.gsg[gs][:, vh * 512:(vh + 1) * 512], in_=scr["gsgT"][2 * j + vh][:, psl]),
                        reads=[f"scr_glg{2 * j + vh}_{p}"], writes=[f"gsg{gs}"], dma_key=f"gsg{gs}_{vh}")
                for tt in range(4):
                    t = 4 * p + tt
                    c0 = tt * 128
                    tm = rtmp.next()
                    gt = self.gt[tm]
                    S.op("pe", lambda g, ls=ls, c0=c0, j=j: g.matmul(
                        self.banks[0][:, 0:128], lhsT=self.lra[ls][0:17, c0:c0 + 128], rhs=wal[0:17, j * 128:(j + 1) * 128],
                        start=True, stop=True),
                        reads=[f"lra{ls}", "wal"], writes=["bank0"])
                    S.op("act", lambda g, gt=gt: g.activation(out=gt["s"][:], in_=self.banks[0][:, 0:128], func=AF.Exp, scale=-1.0),
                         reads=["bank0"], writes=[f"gts{tm}"])
                    S.op("act", lambda g, gt=gt: g.activation(out=gt["s"][:], in_=gt["s"][:], func=AF.Ln, bias=self.onec[:], scale=1.0),
                         reads=[f"gts{tm}", "onec"], writes=[f"gts{tm}"])
                    S.op("pe", lambda g, gt=gt: g.matmul(self.banks[1][:, 0:128], lhsT=gt["s"][:], rhs=self.bt, start=True, stop=True),
                         reads=[f"gts{tm}", "cst"], writes=["bank1"])
                    S.op("act", lambda g, gt=gt: g.activation(out=gt["e1"][:], in_=self.banks[1][:, 0:128], func=AF.Exp, scale=-1.0 / 16),
                         reads=["bank1"], writes=[f"gte1{tm}"])
                    S.op("act", lambda g, gt=gt: g.activation(out=gt["e2"][:], in_=self.banks[1][:, 0:128], func=AF.Exp, scale=1.0 / 16),
                         reads=["bank1"], writes=[f"gte2{tm}"])
                    S.op("dve", lambda g, gt=gt, gs=gs, c0=c0: g.scalar_tensor_tensor(
                        out=gt["qd"][:], in0=self.gq[gs][:, c0:c0 + 128], scalar=float(128 ** -0.5), in1=gt["e1"][:],
                        op0=ALU.mult, op1=ALU.mult),
                        reads=[f"gq{gs}", f"gte1{tm}"], writes=[f"gtqd{tm}"])
                    S.op("dve", lambda g, gt=gt, gs=gs, c0=c0: g.tensor_tensor(
                        out=gt["ki"][:], in0=self.gk[gs][:, c0:c0 + 128], in1=gt["e2"][:], op=ALU.mult),
                        reads=[f"gk{gs}", f"gte2{tm}"], writes=[f"gtki{tm}"])
                    S.op("pe", lambda g, gt=gt: g.matmul(self.banks[2][:, 0:128], lhsT=gt["ki"][:], rhs=gt["qd"][:], start=True, stop=True),
                         reads=[f"gtki{tm}", f"gtqd{tm}"], writes=["bank2"])
                    S.op("dve", lambda g, gt=gt: g.tensor_tensor(out=gt["sm"][:], in0=self.banks[2][:, 0:128], in1=self.maskt, op=ALU.mult),
                         reads=["bank2", "cst"], writes=[f"gtsm{tm}"])
                    S.op("pe", lambda g, gt=gt: g.transpose(out=self.bankT[0][:, 0:128], in_=gt["ki"][:], identity=self.identb[:]),
                         reads=[f"gtki{tm}", "identb"], writes=["bankT0"])
                    S.op("act", lambda g, gt=gt: g.copy(out=gt["kt"][:], in_=self.bankT[0][:, 0:128]),
                         reads=["bankT0"], writes=[f"gtkt{tm}"])
                    for ch in range(2):
                        r0 = ch * 64
                        nxt = 1 - cur
                        for vh in range(2):
                            pob = self.banks[4 + vh]
                            vcol = j * 256 + vh * 128
                            S.op("pe", lambda g, pob=pob, t=t, r0=r0, vcol=vcol, gt=gt, c0=c0: g.matmul(
                                pob[:, c0 + r0:c0 + r0 + 64], lhsT=gvs[r0:r0 + 64, t, vcol:vcol + 128],
                                rhs=gt["sm"][r0:r0 + 64, r0:r0 + 64], start=True, stop=False),
                                reads=["vsb", f"gtsm{tm}"], writes=[f"bank{4 + vh}"])
                            S.op("pe", lambda g, pob=pob, r0=r0, vh=vh, gt=gt, c0=c0, cur=cur: g.matmul(
                                pob[:, c0 + r0:c0 + r0 + 64], lhsT=self.Sb[cur][:, vh * 128:(vh + 1) * 128],
                                rhs=gt["qd"][:, r0:r0 + 64], start=False, stop=True),
                                reads=[f"Sb{cur}", f"gtqd{tm}"], writes=[f"bank{4 + vh}"])
                        S.op("pe", lambda g, t=t, r0=r0, gt=gt, j=j: g.matmul(
                            self.banks[3][:, 0:256], lhsT=gt["kt"][r0:r0 + 64, :], rhs=gvs[r0:r0 + 64, t, j * 256:(j + 1) * 256],
                            start=True, stop=True),
                            reads=[f"gtkt{tm}", "vsb"], writes=["bank3"])
                        dcol = gt["e1"][:, r0 + 63:r0 + 64]
                        S.op("dve", lambda g, cur=cur, dcol=dcol: g.tensor_scalar(
                            out=self.Stmp[:], in0=self.Sf[cur][:], scalar1=dcol, scalar2=None, op0=ALU.mult),
                            reads=[f"Sf{cur}", f"gte1{tm}"], writes=["Stmp"])
                        S.op("dve", lambda g, nxt=nxt, dcol=dcol: g.scalar_tensor_tensor(
                            out=self.Sf[nxt][:], in0=self.banks[3][:, 0:256], scalar=dcol, in1=self.Stmp[:],
                            op0=ALU.mult, op1=ALU.add),
                            reads=["bank3", "Stmp", f"gte1{tm}"], writes=[f"Sf{nxt}"])
                        S.op("act", lambda g, nxt=nxt: g.copy(out=self.Sb[nxt][:], in_=self.Sf[nxt][:]),
                             reads=[f"Sf{nxt}"], writes=[f"Sb{nxt}"])
                        cur = nxt
                for vh in range(2):
                    S.op("act", lambda g, vh=vh: g.activation(out=self.gsq[vh][:], in_=self.banks[4 + vh][:, 0:512], func=AF.Square),
                         reads=[f"bank{4 + vh}"], writes=[f"gsq{vh}"])
                for vh in range(2):
                    S.op("pe", lambda g, vh=vh: g.matmul(self.banks[0][:, 0:512], lhsT=self.ones, rhs=self.gsq[vh][:],
                                                         start=(vh == 0), stop=(vh == 1)),
                         reads=[f"gsq{vh}", "cst"], writes=["bank0"])
                S.op("act", lambda g: g.activation(out=self.gsq[0][:], in_=self.banks[0][:, 0:512], func=AF.Sqrt,
                                                   scale=1.0 / 256, bias=self.epsc[:]),
                     reads=["bank0", "epsc"], writes=["gsq0"])
                S.op("dve", lambda g: g.reciprocal(out=self.gsq[0][:], in_=self.gsq[0][:]), reads=["gsq0"], writes=["gsq0"])
                for vh in range(2):
                    S.op("dve", lambda g, vh=vh, j=j: g.scalar_tensor_tensor(
                        out=self.gsq[1][:], in0=self.banks[4 + vh][:, 0:512], scalar=gog_col[:, 2 * j + vh:2 * j + vh + 1],
                        in1=self.gsq[0][:], op0=ALU.mult, op1=ALU.mult),
                        reads=[f"bank{4 + vh}", "gsq0", "gains"], writes=["gsq1"])
                    ms = rmix.next()
                    S.op("dve", lambda g, vh=vh, gs=gs, ms=ms: g.tensor_tensor(
                        out=self.mixs[ms][:], in0=self.gsq[1][:], in1=self.gsg[gs][:, vh * 512:(vh + 1) * 512], op=ALU.mult),
                        reads=["gsq1", f"gsg{gs}"], writes=[f"mixs{ms}"])
                    row0 = 128 * NSB + j * 256 + vh * 128
                    S.op("sp", lambda g, ms=ms, row0=row0, psl=psl: g.dma_start(
                        out=mixT_ap[row0:row0 + 128, psl], in_=self.mixs[ms][:]),
                        reads=[f"mixs{ms}"], writes=[f"mixT_gl{j}_{vh}_{p}"], dma_key=f"mixs{ms}")

    def phase_outproj(self, L, mix_srcs, wout_ap, xin_ap, xout_ap, TB):
        nc, cfg, S = self.nc, self.cfg, self.S
        FC, DOUT = cfg.FC, cfg.DOUT
        mixr = self.mixr
        row = 0
        fcn = 0
        for src in mix_srcs:
            nrows = src.shape[0]
            nchunk = nrows // 128
            S.op("sp", lambda g, src=src, fcn=fcn, nchunk=nchunk: g.dma_start(
                out=mixr[:, fcn:fcn + nchunk, :], in_=src.rearrange("(k p) t -> p k t", p=128)),
                reads=["mixT_all"], writes=["mixr"], dma_key=f"mixr{fcn}")
            fcn += nchunk
        assert fcn == FC
        wo = self.wb
        rw = Rot(len(wo))
        rbank = Rot(4)
        rxs = Rot(len(self.xo))
        ncb = DOUT // 512
        ntt = TB // 128

        def load_wo(cb):
            s = rw.next()
            step = 4
            for k0 in range(0, FC, step):
                kn = min(step, FC - k0)
                src = wout_ap[k0 * 128:(k0 + kn) * 128, cb * 512:(cb + 1) * 512].rearrange("(k p) c -> p k c", p=128)
                S.op("pool", lambda g, s=s, k0=k0, kn=kn, src=src: g.dma_start(out=wo[s][:, k0:k0 + kn, :], in_=src),
                     writes=[f"wb{s}"], dma_key=f"wb{s}")
            return s

        nxt = load_wo(0)
        for cb in range(ncb):
            slot = nxt
            if cb + 1 < ncb:
                nxt = load_wo(cb + 1)
            for t in range(ntt):
                xs = rxs.next()
                S.op("sp", lambda g, xs=xs, t=t, cb=cb: g.dma_start(
                    out=self.xo[xs][:], in_=xin_ap[t * 128:(t + 1) * 128, cb * 512:(cb + 1) * 512]),
                    writes=[f"xo{xs}"], dma_key=f"xo{xs}")
                bk = rbank.next()
                bank = self.banks[bk]
                for fc in range(FC):
                    S.op("pe", lambda g, bank=bank, fc=fc, t=t, slot=slot: g.matmul(
                        bank[:, 0:512], lhsT=mixr[:, fc, t * 128:(t + 1) * 128], rhs=wo[slot][:, fc, :],
                        start=(fc == 0), stop=(fc == FC - 1)),
                        reads=["mixr", f"wb{slot}"], writes=[f"bank{bk}"])
                S.op("dve", lambda g, bank=bank, xs=xs: g.tensor_tensor(
                    out=self.xo[xs][:], in0=bank[:, 0:512], in1=self.xo[xs][:], op=ALU.add),
                    reads=[f"bank{bk}", f"xo{xs}"], writes=[f"xo{xs}"])
                S.op("sp", lambda g, xs=xs, t=t, cb=cb: g.dma_start(
                    out=xout_ap[t * 128:(t + 1) * 128, cb * 512:(cb + 1) * 512], in_=self.xo[xs][:]),
                    reads=[f"xo{xs}"], writes=[f"xout_{t}_{cb}"], dma_key=f"xst{xs}")

    def flush(self):
        self.S.barrier()
        self.S.flush(lambda nm: self.es.enter_context(self.nc.semaphore(nm)))


def _common(B, consts_ap):
    S = B.S
    B.alloc_common()
    B.epsc = B.sb("epsc", [128, 1], F32)
    B.onec = B.sb("onec", [128, 1], F32)
    B.load_consts(consts_ap)
    S.op("pool", lambda g: g.memset(B.epsc[:], EPS), writes=["epsc"])
    S.op("pool", lambda g: g.memset(B.onec[:], 1.0), writes=["onec"])


def emit_layer_A(B, L, x_ap, prm, scr, mixT_ap):
    nc, cfg, S = B.nc, B.cfg, B.S
    KC, NT, NSB, NGL = cfg.KC, cfg.NT, cfg.NSB, cfg.NGL
    es_outer = B.es
    gq = B.sb(f"gq{L}", [128, 1], F32)
    gk = B.sb(f"gk{L}", [128, 1], F32)
    og = B.sb(f"og{L}", [128, NSB], F32)
    gog = B.sb(f"gog{L}", [128, 2 * NGL], F32)
    wal = B.sb(f"wal{L}", [17, 128 * NGL], F32)
    gd = []
    gd.append(S.op("sp", lambda g: g.dma_start(out=gq[:], in_=prm["qg"]), dma_key="gains"))
    gd.append(S.op("sp", lambda g: g.dma_start(out=gk[:], in_=prm["kg"]), dma_key="gains"))
    gd.append(S.op("sp", lambda g: g.dma_start(out=og[:], in_=prm["og"]), dma_key="gains"))
    gd.append(S.op("sp", lambda g: g.dma_start(out=gog[:], in_=prm["gog"]), dma_key="gains"))
    gd.append(S.op("sp", lambda g: g.dma_start(out=wal[:], in_=prm["wal"]), dma_key="gains"))
    S.op("act", lambda g: g.mul(out=gq[:], in_=gq[:], mul=float(128 ** -0.5)), writes=["gains", "wal"], extra_deps=gd)
    with ExitStack() as es1:
        B.es = es1
        hT = HT([B.sb(f"hT{a}", [128, min(8, KC - 8 * a), cfg.S], BF16) for a in range((KC + 7) // 8)])
        with ExitStack() as es2:
            B.es = es2
            B.gbc = B.sb("gbc", [128, cfg.D], F32)
            B.xt = [B.sb(f"xt{i}", [128, cfg.D], F32) for i in range(2)]
            B.hb = [B.sb(f"hb{i}", [128, cfg.D], BF16) for i in range(2)]
            B.junk = B.sb("junk", [128, cfg.D], BF16)
            B.nstat = B.sb("nstat", [128, 3 * NT], F32)
            B.es = es_outer
            B.phase_norm(L, x_ap, prm["ngain"], hT)
            B.flush()
        with ExitStack() as es2:
            B.es = es2
            B.wb = [B.sb(f"wb{i}", [128, KC, 512], BF16) for i in range(2)]
            B.sq = [B.sb(f"sq{i}", [128, 512], F32) for i in range(2)]
            B.stgb = [B.sb(f"stgb{i}", [128, 512], BF16) for i in range(4)]
            B.stgf = [B.sb(f"stgf{i}", [128, 512], F32) for i in range(3)]
            B.es = es_outer
            B.phase_inproj(L, prm["win"], hT, scr, gq[:], gk[:])
            B.flush()
    with ExitStack() as es1:
        B.es = es1
        Sq = cfg.S
        B.vsb = B.sb("vsb", [128, NT, max(128 * NSB, 256 * NGL)], BF16)
        B.qn = [B.sb(f"qn{i}", [128, Sq], BF16) for i in range(2)]
        B.kn = [B.sb(f"kn{i}", [128, Sq], BF16) for i in range(2)]
        B.sg = [B.sb(f"sg{i}", [128, Sq], BF16) for i in range(2)]
        B.cs = [B.sb(f"cs{i}", [128, Sq + 4], F32) for i in range(2)]
        B.wt = [B.sb(f"wt{i}", [128, Sq], BF16) for i in range(2)]
        B.ebuf = [B.sb(f"ebuf{i}", [128, 512], F32) for i in range(3)]
        B.wT = [B.sb(f"wT{i}", [128, 512], BF16) for i in range(4)]
        B.sqo = B.sb("sqo", [128, 512], F32)
        B.mixs = [B.sb(f"mixs{i}", [128, 512], BF16) for i in range(3)]
        B.rmix = Rot(3)
        B.djunk = B.sb("djunk", [128, 128], F32)
        B.ntot = B.sb("ntot", [128, 2], F32)
        B.lra = [B.sb(f"lra{i}", [17, 512], F32) for i in range(2)]
        B.gq = [B.sb(f"gq_{i}", [128, 512], F32) for i in range(2)]
        B.gk = [B.sb(f"gk_{i}", [128, 512], F32) for i in range(2)]
        B.gsg = [B.sb(f"gsg{i}", [128, 1024], BF16) for i in range(2)]
        B.gt = []
        for i in range(2):
            B.gt.append(dict(
                s=B.sb(f"gts{i}", [128, 128], F32), e1=B.sb(f"gte1{i}", [128, 128], F32),
                e2=B.sb(f"gte2{i}", [128, 128], F32), qd=B.sb(f"gtqd{i}", [128, 128], BF16),
                ki=B.sb(f"gtki{i}", [128, 128], BF16), sm=B.sb(f"gtsm{i}", [128, 128], BF16),
                kt=B.sb(f"gtkt{i}", [128, 128], BF16)))
        B.Sf = [B.sb(f"Sf{i}", [128, 256], F32) for i in range(2)]
        B.Sb = [B.sb(f"Sb{i}", [128, 256], BF16) for i in range(2)]
        B.Stmp = B.sb("Stmp", [128, 256], F32)
        B.gsq = [B.sb(f"gsq{i}", [128, 512], F32) for i in range(2)]
        B.es = es_outer
        B.phase_sb(L, scr, og, mixT_ap)
        B.phase_gla(L, scr, wal, gog, mixT_ap)
        B.flush()
    B.es = es_outer


def make_scratch(B, tag=""):
    cfg = B.cfg
    NSB, NGL, Sq, NT = cfg.NSB, cfg.NGL, cfg.S, cfg.NT
    scr = {}
    scr["qT"] = [B.dram(f"scr_qT{tag}{h}", [128, Sq], BF16) for h in range(NSB)]
    scr["kT"] = [B.dram(f"scr_kT{tag}{h}", [128, Sq], BF16) for h in range(NSB)]
    scr["sgT"] = [B.dram(f"scr_sgT{tag}{h}", [128, Sq], BF16) for h in range(NSB)]
    scr["sbv"] = B.dram(f"scr_sbv{tag}", [128, NT, 128 * NSB], BF16)
    scr["gqT"] = [B.dram(f"scr_gqT{tag}{j}", [128, Sq], F32) for j in range(NGL)]
    scr["gkT"] = [B.dram(f"scr_gkT{tag}{j}", [128, Sq], F32) for j in range(NGL)]
    scr["gsgT"] = [B.dram(f"scr_gsgT{tag}{j}", [128, Sq], BF16) for j in range(2 * NGL)]
    scr["glv"] = B.dram(f"scr_glv{tag}", [128, NT, 256 * NGL], BF16)
    scr["lrT"] = B.dram(f"scr_lrT{tag}", [16, Sq], F32)
    return scr


def build_A(cfg):
    nc = bass.Bass("TRN2", target_bir_lowering=False)
    NSB, NGL = cfg.NSB, cfg.NGL
    ext = lambda n, s, d: nc.dram_tensor(n, s, d, kind="ExternalInput").ap()
    x = ext("x", [cfg.S, cfg.D], F32)
    prm = dict(win=ext("win", [cfg.D, cfg.NCOL], F32), ngain=ext("ngain", [1, cfg.D], F32),
               qg=ext("qg", [128, 1], F32), kg=ext("kg", [128, 1], F32), og=ext("og", [128, NSB], F32),
               gog=ext("gog", [128, 2 * NGL], F32), wal=ext("wal", [17, 128 * NGL], F32))
    consts = ext("consts", [128, 640], F32)
    mixT = nc.dram_tensor("mixT", [cfg.MIXW, cfg.S], BF16, kind="ExternalOutput").ap()
    with ExitStack() as es:
        B = Builder(nc, cfg, es)
        _common(B, consts)
        scr = make_scratch(B)
        emit_layer_A(B, 0, x, prm, scr, mixT)
    return nc


def build_B(cfg):
    nc = bass.Bass("TRN2", target_bir_lowering=False)
    ext = lambda n, s, d: nc.dram_tensor(n, s, d, kind="ExternalInput").ap()
    TB, FC, DOUT = cfg.TB, cfg.FC, cfg.DOUT
    mix = ext("mix", [FC * 128, TB], BF16)
    wout = ext("wout", [FC * 128, DOUT], F32)
    xin = ext("xin", [TB, DOUT], F32)
    xout = nc.dram_tensor("xout", [TB, DOUT], F32, kind="ExternalOutput").ap()
    with ExitStack() as es:
        B = Builder(nc, cfg, es)
        B.banks = [B.ps(f"bank{i}", [128, 512], F32) for i in range(4)]
        B.mixr = B.sb("mixr", [128, FC, TB], BF16)
        B.wb = [B.sb(f"wb{i}", [128, FC, 512], BF16) for i in range(2)]
        B.xo = [B.sb(f"xo{i}", [128, 512], F32) for i in range(4)]
        B.phase_outproj(0, [mix], wout, xin, xout, TB)
        B.flush()
    return nc


from concourse.bass_utils import run_bass_kernel_spmd

N_CORES = 8
_CACHE = {}


def _core_cols(c):
    idx = []
    for i in range(4):
        hh = 4 * c + i
        idx += [np.arange(hh * 128, (hh + 1) * 128), np.arange(1024 + hh * 128, 1024 + (hh + 1) * 128),
                np.arange(3072 + hh * 128, 3072 + (hh + 1) * 128)]
    for j in range(2):
        jj = 2 * c + j
        idx += [np.arange(4096 + jj * 128, 4096 + (jj + 1) * 128), np.arange(4608 + jj * 128, 4608 + (jj + 1) * 128),
                np.arange(6144 + jj * 256, 6144 + (jj + 1) * 256)]
    idx.append(np.arange(2048 + 4 * c * 128, 2048 + (4 * c + 4) * 128))
    idx.append(np.arange(5120 + 2 * c * 256, 5120 + (2 * c + 2) * 256))
    idx.append(np.arange(7168, 7184))
    return np.concatenate(idx)


def _mix_perm():
    p = []
    for c in range(2):
        p.append(np.arange(c * 512, (c + 1) * 512))
        p.append(np.arange(1024 + c * 512, 1024 + (c + 1) * 512))
    return np.concatenate(p)


def _layer_params(l, c, norm_gain, w_in, sb_q_gain, sb_k_gain, sb_o_gain, gla_w_alpha, gla_b_alpha, gla_o_gain):
    f = np.float32
    win = np.ascontiguousarray(w_in[l][:, _core_cols(c)], dtype=f)
    og = np.ascontiguousarray(sb_o_gain[l][c * 512:(c + 1) * 512].reshape(4, 128).T, dtype=f)
    gog = np.ascontiguousarray(gla_o_gain[l][c * 512:(c + 1) * 512].reshape(4, 128).T, dtype=f)
    cols = slice(c * 256, (c + 1) * 256)
    wal = np.ascontiguousarray(np.concatenate([gla_w_alpha[l][:, cols], gla_b_alpha[l][cols][None, :]], axis=0), dtype=f)
    return dict(win=win, ngain=np.ascontiguousarray(norm_gain[l][None, :], dtype=f),
                qg=np.ascontiguousarray(sb_q_gain[l][:, None], dtype=f),
                kg=np.ascontiguousarray(sb_k_gain[l][:, None], dtype=f), og=og, gog=gog, wal=wal)


def kernel(x, norm_gain, w_in, sb_q_gain, sb_k_gain, sb_o_gain, gla_w_alpha, gla_b_alpha, gla_o_gain, w_out):
    args = [np.asarray(a) for a in (x, norm_gain, w_in, sb_q_gain, sb_k_gain, sb_o_gain, gla_w_alpha,
                                    gla_b_alpha, gla_o_gain, w_out)]
    x, norm_gain, w_in, sb_q_gain, sb_k_gain, sb_o_gain, gla_w_alpha, gla_b_alpha, gla_o_gain, w_out = args
    cfg = Cfg()
    if "A" not in _CACHE:
        _CACHE["A"] = build_A(cfg)
        _CACHE["B"] = build_B(cfg)
    ncA, ncB = _CACHE["A"], _CACHE["B"]
    consts = make_consts()
    perm = _mix_perm()
    xcur = np.ascontiguousarray(x, dtype=np.float32)
    cores = list(range(N_CORES))
    for l in range(2):
        in_maps = []
        for core in cores:
            b, c = core // 2, core % 2
            m = _layer_params(l, c, norm_gain, w_in, sb_q_gain, sb_k_gain, sb_o_gain, gla_w_alpha, gla_b_alpha, gla_o_gain)
            m["x"] = xcur[b]
            m["consts"] = consts
            in_maps.append(m)
        resA = run_bass_kernel_spmd(ncA, in_maps, core_ids=cores)
        mixT = [np.asarray(r["mixT"]) for r in resA.results]
        woutp = np.ascontiguousarray(w_out[l][perm], dtype=np.float32)
        in_maps = []
        for core in cores:
            b, c = core // 2, core % 2
            tsl = slice(c * 2048, (c + 1) * 2048)
            mix = np.ascontiguousarray(np.concatenate([mixT[2 * b][:, tsl], mixT[2 * b + 1][:, tsl]], axis=0))
            in_maps.append(dict(mix=mix, wout=woutp, xin=np.ascontiguousarray(xcur[b, tsl])))
        resB = run_bass_kernel_spmd(ncB, in_maps, core_ids=cores)
        xn = np.empty_like(xcur)
        for core in cores:
            b, c = core // 2, core % 2
            xn[b, c * 2048:(c + 1) * 2048] = np.asarray(resB.results[core]["xout"])
        xcur = xn
    return xcur
```
